# Optimizing a Trainium2 kernel written in Bass

```python
import math
import jax, jax.numpy as jnp
from jax import lax
import numpy as np

D_MODEL = 1024
BATCH = 16
SEQ = 4096
DEPTH = 1

EXPAND = 2
D_MIX = EXPAND * D_MODEL
D_SSD = D_MIX // 2
D_MLA = D_MIX - D_SSD
SSD_HEAD_DIM = 64
SSD_HEADS = D_SSD // SSD_HEAD_DIM
SSD_GROUPS = 2
SSD_HPG = SSD_HEADS // SSD_GROUPS
SSD_STATE = 128
CONV_WIDTH = 4
CHUNK = 128
MLA_HEADS = 8
QK_NOPE = 64
QK_ROPE = 32
QK_DIM = QK_NOPE + QK_ROPE
V_HEAD = D_MLA // MLA_HEADS
Q_LORA = 384
KV_LORA = 256
ROPE_THETA = 10000.0
Q_BLOCK = 128
D_FF = 2816
N_MOD = 9
EPS = 1e-6
D_CONV = D_SSD + 2 * SSD_GROUPS * SSD_STATE
IN_WIDTHS = (D_SSD, D_SSD, SSD_GROUPS * SSD_STATE, SSD_GROUPS * SSD_STATE,
             SSD_HEADS, Q_LORA, KV_LORA, QK_ROPE)
D_IN_PROJ = sum(IN_WIDTHS)
IN_SPLITS = tuple(int(v) for v in np.cumsum(IN_WIDTHS)[:-1])

kernel_name = "hymba_ssd_mla_macaron_adaln"


def rmsnorm(x, w):
    xf = x.astype(jnp.float32)
    y = xf * lax.rsqrt(jnp.mean(xf * xf, axis=-1, keepdims=True) + EPS)
    return (y * w.astype(jnp.float32)).astype(x.dtype)


def modulate(h, shift, scale):
    return h * (1.0 + scale[:, None, :]) + shift[:, None, :]


def swiglu(h, w_gate, w_up, w_down):
    return (jax.nn.silu(h @ w_gate) * (h @ w_up)) @ w_down


def apply_rope(u, cos, sin):
    u1, u2 = jnp.split(u, 2, axis=-1)
    return jnp.concatenate([u1 * cos - u2 * sin, u2 * cos + u1 * sin], axis=-1)


def causal_depthwise_conv(u, w, b):
    out = lax.conv_general_dilated(
        u, w[:, None, :].astype(u.dtype), window_strides=(1,),
        padding=[(CONV_WIDTH - 1, 0)],
        dimension_numbers=('NWC', 'WIO', 'NWC'),
        feature_group_count=u.shape[-1])
    return out + b.astype(u.dtype)


def ssd_chunked(xh, dt, A, Bm, Cm):
    b, S = xh.shape[0], xh.shape[1]
    nc = S // CHUNK
    dtype = xh.dtype
    xdt = (xh * dt[..., None].astype(dtype)).reshape(b, nc, CHUNK, SSD_GROUPS, SSD_HPG, SSD_HEAD_DIM)
    Bc = Bm.reshape(b, nc, CHUNK, SSD_GROUPS, SSD_STATE)
    Cc = Cm.reshape(b, nc, CHUNK, SSD_GROUPS, SSD_STATE)
    a_cum = jnp.cumsum((dt * A).reshape(b, nc, CHUNK, SSD_GROUPS, SSD_HPG), axis=2)
    seg = a_cum[:, :, :, None] - a_cum[:, :, None, :]
    causal = jnp.tril(jnp.ones((CHUNK, CHUNK), dtype=bool))[None, None, :, :, None, None]
    Lmat = jnp.exp(jnp.where(causal, seg, -jnp.inf)).astype(dtype)
    cb = jnp.einsum('bclgn,bcsgn->bclsg', Cc, Bc)
    y_diag = jnp.einsum('bclsg,bclsgr,bcsgrp->bclgrp', cb, Lmat, xdt)
    decay_states = jnp.exp(a_cum[:, :, -1:] - a_cum).astype(dtype)
    states = jnp.einsum('bclgn,bclgr,bclgrp->bcgrpn', Bc, decay_states, xdt)
    chunk_decay = jnp.exp(a_cum[:, :, -1]).astype(dtype)

    def step(h, inp):
        s_c, d_c = inp
        return d_c[..., None, None] * h + s_c, h

    h0 = jnp.zeros_like(states[:, 0])
    _, prev = lax.scan(step, h0, (jnp.moveaxis(states, 1, 0), jnp.moveaxis(chunk_decay, 1, 0)))
    prev = jnp.moveaxis(prev, 0, 1)
    y_off = jnp.einsum('bclgn,bcgrpn,bclgr->bclgrp', Cc, prev, jnp.exp(a_cum).astype(dtype))
    return (y_diag + y_off).reshape(b, S, SSD_GROUPS, SSD_HPG, SSD_HEAD_DIM)


def mla_causal_attention(q_nope, q_rope, k_nope, k_rope, v):
    b, S = q_nope.shape[0], q_nope.shape[1]
    nblk = S // Q_BLOCK
    scale = 1.0 / math.sqrt(QK_DIM)
    qn_b = q_nope.reshape(b, nblk, Q_BLOCK, MLA_HEADS, QK_NOPE).swapaxes(0, 1)
    qr_b = q_rope.reshape(b, nblk, Q_BLOCK, MLA_HEADS, QK_ROPE).swapaxes(0, 1)
    starts = jnp.arange(nblk, dtype=jnp.int32) * Q_BLOCK
    k_idx = jnp.arange(S, dtype=jnp.int32)

    def one_block(args):
        qn, qr, start = args
        s = (jnp.einsum('bqhd,bkhd->bhqk', qn, k_nope)
             + jnp.einsum('bqhr,bkr->bhqk', qr, k_rope)).astype(jnp.float32) * scale
        q_idx = start + jnp.arange(Q_BLOCK, dtype=jnp.int32)
        mask = k_idx[None, :] <= q_idx[:, None]
        s = jnp.where(mask[None, None], s, -jnp.inf)
        p = jax.nn.softmax(s, axis=-1).astype(v.dtype)
        return jnp.einsum('bhqk,bkhv->bqhv', p, v)

    out = lax.map(one_block, (qn_b, qr_b, starts))
    return out.swapaxes(0, 1).reshape(b, S, MLA_HEADS * V_HEAD)


def hybrid_mixer(h, positions, w_in, conv_w, conv_b, dt_bias, a_log, d_skip, ssd_norm_w,
                 q_norm_w, w_uq, kv_norm_w, w_ukv, mla_norm_w, w_out):
    b, S = h.shape[0], h.shape[1]
    proj = h @ w_in
    z, xs, Bm, Cm, dt_raw, cq, ckv, k_rope_raw = jnp.split(proj, IN_SPLITS, axis=-1)

    xBC = jax.nn.silu(causal_depthwise_conv(jnp.concatenate([xs, Bm, Cm], axis=-1), conv_w, conv_b))
    xs, Bm, Cm = jnp.split(xBC, [D_SSD, D_SSD + SSD_GROUPS * SSD_STATE], axis=-1)
    dt = jax.nn.softplus(dt_raw.astype(jnp.float32) + dt_bias.astype(jnp.float32))
    A = -jnp.exp(a_log.astype(jnp.float32))
    xh = xs.reshape(b, S, SSD_GROUPS, SSD_HPG, SSD_HEAD_DIM)
    y = ssd_chunked(xh, dt.reshape(b, S, SSD_GROUPS, SSD_HPG), A.reshape(SSD_GROUPS, SSD_HPG),
                    Bm.reshape(b, S, SSD_GROUPS, SSD_STATE), Cm.reshape(b, S, SSD_GROUPS, SSD_STATE))
    y = y + d_skip.reshape(SSD_GROUPS, SSD_HPG)[:, :, None].astype(y.dtype) * xh
    yg = (y.reshape(b, S, D_SSD) * jax.nn.silu(z)).reshape(b, S, SSD_GROUPS, D_SSD // SSD_GROUPS)
    y_ssd = rmsnorm(yg, ssd_norm_w.reshape(SSD_GROUPS, D_SSD // SSD_GROUPS)).reshape(b, S, D_SSD)

    q = (rmsnorm(cq, q_norm_w) @ w_uq).reshape(b, S, MLA_HEADS, QK_DIM)
    q_nope, q_rope = jnp.split(q, [QK_NOPE], axis=-1)
    kv = (rmsnorm(ckv, kv_norm_w) @ w_ukv).reshape(b, S, MLA_HEADS, QK_NOPE + V_HEAD)
    k_nope, v = jnp.split(kv, [QK_NOPE], axis=-1)
    inv_freq = ROPE_THETA ** (-jnp.arange(0, QK_ROPE, 2, dtype=jnp.float32) / QK_ROPE)
    ang = positions[..., None].astype(jnp.float32) * inv_freq
    cos, sin = jnp.cos(ang).astype(h.dtype), jnp.sin(ang).astype(h.dtype)
    q_rope = apply_rope(q_rope, cos[:, :, None], sin[:, :, None])
    k_rope = apply_rope(k_rope_raw, cos, sin)
    attn = mla_causal_attention(q_nope, q_rope, k_nope, k_rope, v)
    y_mla = rmsnorm(attn, mla_norm_w)

    return jnp.concatenate([y_ssd, y_mla], axis=-1) @ w_out


def setup_inputs(seed: int = 0) -> dict:
    key = jax.random.key(seed)
    ks = iter(jax.random.split(key, 40))

    def dense(shape, fan_in):
        return jax.random.normal(next(ks), shape, jnp.float32) * fan_in ** -0.5

    def gain(shape):
        return 1.0 + 0.05 * jax.random.normal(next(ks), shape, jnp.float32)

    def small(shape, s=0.02):
        return s * jax.random.normal(next(ks), shape, jnp.float32)

    L = DEPTH
    x = jax.random.normal(next(ks), (BATCH, SEQ, D_MODEL), jnp.float32)
    c = jax.random.normal(next(ks), (BATCH, D_MODEL), jnp.float32)
    offsets = jax.random.randint(next(ks), (BATCH, 1), 0, 1024, dtype=jnp.int32)
    positions = offsets + jnp.arange(SEQ, dtype=jnp.int32)[None, :]
    dt0 = jnp.exp(jax.random.uniform(next(ks), (L, SSD_HEADS), jnp.float32,
                                     math.log(1e-3), math.log(1e-1)))
    dt_bias = dt0 + jnp.log(-jnp.expm1(-dt0))
    a_log = jnp.log(jax.random.uniform(next(ks), (L, SSD_HEADS), jnp.float32, 1.0, 16.0))
    return {
        "x": x,
        "c": c,
        "positions": positions,
        "w_ada": dense((L, D_MODEL, N_MOD * D_MODEL), D_MODEL),
        "b_ada": small((L, N_MOD * D_MODEL)),
        "norm_ffn1": gain((L, D_MODEL)),
        "ffn1_w_gate": dense((L, D_MODEL, D_FF), D_MODEL),
        "ffn1_w_up": dense((L, D_MODEL, D_FF), D_MODEL),
        "ffn1_w_down": dense((L, D_FF, D_MODEL), D_FF),
        "norm_mix": gain((L, D_MODEL)),
        "w_in": dense((L, D_MODEL, D_IN_PROJ), D_MODEL),
        "conv_w": dense((L, CONV_WIDTH, D_CONV), CONV_WIDTH),
        "conv_b": small((L, D_CONV)),
        "dt_bias": dt_bias,
        "a_log": a_log,
        "d_skip": gain((L, SSD_HEADS)),
        "ssd_norm_w": gain((L, D_SSD)),
        "q_norm_w": gain((L, Q_LORA)),
        "w_uq": dense((L, Q_LORA, MLA_HEADS * QK_DIM), Q_LORA),
        "kv_norm_w": gain((L, KV_LORA)),
        "w_ukv": dense((L, KV_LORA, MLA_HEADS * (QK_NOPE + V_HEAD)), KV_LORA),
        "mla_norm_w": gain((L, D_MLA)),
        "w_out": dense((L, D_MIX, D_MODEL), D_MIX),
        "norm_ffn2": gain((L, D_MODEL)),
        "ffn2_w_gate": dense((L, D_MODEL, D_FF), D_MODEL),
        "ffn2_w_up": dense((L, D_MODEL, D_FF), D_MODEL),
        "ffn2_w_down": dense((L, D_FF, D_MODEL), D_FF),
        "norm_final": gain((D_MODEL,)),
    }


def reference(x, c, positions, w_ada, b_ada, norm_ffn1, ffn1_w_gate, ffn1_w_up, ffn1_w_down,
              norm_mix, w_in, conv_w, conv_b, dt_bias, a_log, d_skip, ssd_norm_w,
              q_norm_w, w_uq, kv_norm_w, w_ukv, mla_norm_w, w_out,
              norm_ffn2, ffn2_w_gate, ffn2_w_up, ffn2_w_down, norm_final):
    c_act = jax.nn.silu(c)
    for l in range(DEPTH):
        mod = c_act @ w_ada[l] + b_ada[l]
        (sh1, sc1, g1, sh2, sc2, g2, sh3, sc3, g3) = jnp.split(mod, N_MOD, axis=-1)
        h = modulate(rmsnorm(x, norm_ffn1[l]), sh1, sc1)
        x = x + 0.5 * g1[:, None, :] * swiglu(h, ffn1_w_gate[l], ffn1_w_up[l], ffn1_w_down[l])
        h = modulate(rmsnorm(x, norm_mix[l]), sh2, sc2)
        x = x + g2[:, None, :] * hybrid_mixer(
            h, positions, w_in[l], conv_w[l], conv_b[l], dt_bias[l], a_log[l], d_skip[l],
            ssd_norm_w[l], q_norm_w[l], w_uq[l], kv_norm_w[l], w_ukv[l], mla_norm_w[l], w_out[l])
        h = modulate(rmsnorm(x, norm_ffn2[l]), sh3, sc3)
        x = x + 0.5 * g3[:, None, :] * swiglu(h, ffn2_w_gate[l], ffn2_w_up[l], ffn2_w_down[l])
    return rmsnorm(x, norm_final)
```

```python
import math
from contextlib import ExitStack
import numpy as np
import concourse.bass as bass
import concourse.mybir as mybir
from concourse.bass_utils import run_bass_kernel_spmd

F32 = mybir.dt.float32
BF16 = mybir.dt.bfloat16
I32 = mybir.dt.int32
AF = mybir.ActivationFunctionType
ALU = mybir.AluOpType
ESZ = {F32: 4, BF16: 2, I32: 4}

D = 1024
KD = 8
DFF = 2816
NHC = 22
NHG = 11
T = 512
NSUB = 4
PIECE = 4096
NPIECE = 48
PG = 256
EPS = 1e-6
NRING = 7
QSCALE = 1.0 / math.sqrt(96.0)

C_BADA = 0
C_NF = 72
C_CONVW = 96
C_CONVB = 144
C_QKVN = 156
C_MLAW = 161
C_SSDW = 169
C_DTB = 177
C_ALOG = 193
C_DSK = 209
C_INVF = 225
C_PHASE = 226
C_DFEAT = 228
NCST = 236
M_ID = 0
M_U = 128
M_SU = 256
M_ONE = 384
M_RQ = 512
NCM = 608


class Sem:
    def __init__(self, h, dma):
        self.h = h
        self.cnt = 0
        self.dma = dma


class Res:
    def __init__(self, psum=False):
        self.lw = None
        self.rd = {}
        self.psum = psum


class PEProxy:
    def __init__(self, h, prog):
        self._h = h
        self._p = prog

    def matmul(self, out, lhsT=None, rhs=None, **kw):
        self._p.pe_tags.append((self._p.stage, 2 if lhsT.dtype == F32 else 1))
        return self._h.matmul(out, lhsT=lhsT, rhs=rhs, **kw)

    def transpose(self, **kw):
        self._p.pe_tags.append((self._p.stage, 1))
        return self._h.transpose(**kw)

    def __getattr__(self, k):
        return getattr(self._h, k)


class Prog:
    def __init__(self, S, NB, dbg_cols=0, stop=None):
        self.S = S
        self.NB = NB
        self.NT = S // T
        self.stop = stop
        self.dbg_cols = dbg_cols
        self.es = ExitStack()
        nc = self.nc = bass.Bass("TRN2", target_bir_lowering=False)
        dr = lambda n, sh, dt, kind: nc.dram_tensor(n, sh, dt, kind=kind).ap()
        self.x_d = dr("x", [NB, S, D], F32, "ExternalInput")
        self.pos_d = dr("pos", [NB, S], I32, "ExternalInput")
        self.cT_d = dr("cT", [128, KD * NB], F32, "ExternalInput")
        self.cst_d = dr("cst", [128, NCST], F32, "ExternalInput")
        self.cmat_d = dr("cmat", [128, NCM], F32, "ExternalInput")
        self.nfb_d = dr("nfb", [128, D], F32, "ExternalInput")
        self.wada_d = dr("wada", [18, 128, PIECE], F32, "ExternalInput")
        self.wbig_d = dr("wbig", [NPIECE, 128, PIECE], F32, "ExternalInput")
        self.out_d = dr("out", [NB, S, D], F32, "ExternalOutput")
        self.wsc_d = dr("wsc", [NPIECE, 128, PIECE], BF16, "Internal")
        self.kc_d = dr("kc", [NB, 8, 96, S], BF16, "Internal")
        self.vc_d = dr("vc", [NB, 8, 128, (S // 128) * 128], BF16, "Internal")
        if dbg_cols:
            self.dbg_d = dr("dbg", [128, dbg_cols], F32, "ExternalOutput")
        self.NA = 101376
        self.arena = self.es.enter_context(nc.sbuf_tensor("arena", [128, self.NA], BF16))
        self.pst = self.es.enter_context(nc.psum_tensor("pst", [128, 4096], F32))
        self.pages = {}
        self.eng = {}
        self.pe_tags = []
        for n, h in (("pe", PEProxy(nc.tensor, self)), ("act", nc.scalar), ("dve", nc.vector), ("pool", nc.gpsimd), ("sp", nc.sync)):
            self.eng[n] = (h, self.newsem("e_" + n, False), {})
        self.off = 0
        self.rec = None
        self.tags = []
        self.stage = "setup"
        self.phase0 = None
        self.live = {}
        self.peak = 0
        self.nload = 0
        self.nuse = 0
        self.nrel = 0

    def newsem(self, name, dma=True):
        return Sem(self.es.enter_context(self.nc.semaphore(name)), dma)

    def take(self, nbytes):
        nbytes = (nbytes + PG - 1) // PG * PG
        if self.phase0 is None:
            o = self.off
            self.off = o + nbytes
            assert self.off <= self.NA * 2, ("SBUF overflow", self.off)
            return o
        cur = self.phase0
        for (o, n) in sorted(self.live.items()):
            if o - cur >= nbytes:
                break
            cur = max(cur, o + n)
        assert cur + nbytes <= self.NA * 2, ("SBUF overflow", cur, nbytes, sorted(self.live.items()))
        self.live[cur] = nbytes
        self.peak = max(self.peak, cur + nbytes)
        return cur

    def free(self, *aps):
        for a in aps:
            esz = ESZ[a.dtype]
            o = (a.offset % a.ap[0][0]) * esz
            del self.live[o]

    def buf(self, shape, dt, parts=128):
        n = int(np.prod(shape))
        o = self.take(n * ESZ[dt])
        return self.view(o, shape, dt, parts)

    def view(self, o, shape, dt, parts=128):
        n = int(np.prod(shape))
        ap = self.arena[0:parts, o // 2: o // 2 + n * ESZ[dt] // 2]
        if dt != BF16:
            ap = ap.bitcast(dt)
        if len(shape) == 2:
            ap = ap.rearrange("p (a b) -> p a b", a=shape[0])
        elif len(shape) == 3:
            ap = ap.rearrange("p (a b c) -> p a b c", a=shape[0], b=shape[1])
        return ap

    def bank(self, i, dt=F32):
        ap = self.pst[:, i * 512:(i + 1) * 512]
        if dt == BF16:
            ap = ap.bitcast(BF16)
        return ap

    def _keys(self, a):
        if isinstance(a, Res):
            return [a]
        esz = ESZ[a.dtype]
        pstride = a.ap[0][0]
        off = a.offset % pstride
        ext = 1 + sum((c - 1) * abs(s) for s, c in a.ap[1:])
        b0 = off * esz
        b1 = (off + ext) * esz
        sp = 0 if a.tensor.name == "arena" else 1
        out = []
        pgsz = PG if sp == 0 else 2048
        for i in range(b0 // pgsz, (b1 - 1) // pgsz + 1):
            k = (sp, i)
            r = self.pages.get(k)
            if r is None:
                r = self.pages[k] = Res(psum=(sp == 1))
            out.append(r)
        return out

    def _sync(self, en, reads, writes, strict=False):
        h, own, waited = self.eng[en]
        need = {}
        rk = [r for a in reads for r in self._keys(a)]
        wk = [r for a in writes for r in self._keys(a)]
        for r in rk:
            if r.lw is not None:
                s, v = r.lw
                if need.get(s, 0) < v:
                    need[s] = v
            if r.psum:
                for s, v in r.rd.items():
                    if s is not own and need.get(s, 0) < v:
                        need[s] = v
        pe = (en == "pe")
        for r in wk:
            if r.lw is not None:
                s, v = r.lw
                if not (pe and s is own) and need.get(s, 0) < v:
                    need[s] = v
            for s, v in r.rd.items():
                if not (pe and s is own) and need.get(s, 0) < v:
                    need[s] = v
        for s, v in need.items():
            if s.dma:
                v = s.cnt
            if waited.get(s, 0) >= v:
                continue
            h.wait_ge(s.h, v)
            waited[s] = v
        return rk, wk

    def _rec(self, ev, rk, wk):
        s, v = ev
        for r in wk:
            r.lw = ev
            r.rd = {}
        for r in rk:
            if r.rd.get(s, 0) < v:
                r.rd[s] = v

    def op(self, en, fn, reads=(), writes=(), cost=None):
        if self.rec is not None:
            self.rec.append((en, fn, list(reads), list(writes), cost, self.stage))
            return
        h, own, _ = self.eng[en]
        rk, wk = self._sync(en, reads, writes)
        ins = fn(h)
        self.tags.append((en, self.stage))
        own.cnt += 1
        ins.then_inc(own.h, 1)
        self._rec((own, own.cnt), rk, wk)

    def dma(self, en, out, in_, sem, reads=(), writes=(), **kw):
        h, own, _ = self.eng[en]
        rk, wk = self._sync(en, reads, writes, strict=True)
        ins = h.dma_start(out=out, in_=in_, **kw)
        sem.cnt += 16
        ins.then_inc(sem.h, 16)
        self._rec((sem, sem.cnt), rk, wk)


    def rec_begin(self):
        assert self.rec is None
        self.rec = []

    def _cost(self, en, writes, cost):
        if cost is not None:
            return cost
        a = writes[0]
        n = 1
        for st, c in a.ap[1:]:
            n *= c
        if en == "dve":
            return 0.08 + 0.0011 * n
        if en == "pool":
            return 0.12 + 0.0019 * n
        if en == "act":
            return 0.15 + 0.00095 * n
        return 0.3

    def rec_end(self, hop=0.35):
        ops = self.rec
        self.rec = None
        n = len(ops)
        lw = {}
        rd = {}
        deps = [set() for _ in range(n)]
        for i, (en, fn, reads, writes, cost, stg) in enumerate(ops):
            rk = [id(r) for a in reads for r in self._keys(a)]
            wk = [id(r) for a in writes for r in self._keys(a)]
            psum = set(id(r) for a in list(reads) + list(writes) for r in self._keys(a) if r.psum)
            for k in rk:
                if k in lw:
                    deps[i].add(lw[k])
                if k in psum:
                    for j in rd.get(k, ()):
                        if ops[j][0] != en:
                            deps[i].add(j)
            for k in wk:
                if k in lw:
                    deps[i].add(lw[k])
                for j in rd.get(k, ()):
                    deps[i].add(j)
            for k in wk:
                lw[k] = i
                rd[k] = []
            for k in rk:
                rd.setdefault(k, []).append(i)
            deps[i].discard(i)
        dur = [self._cost(o[0], o[3], o[4]) for o in ops]
        end = [None] * n
        free = {}
        done = [False] * n
        order = []
        remaining = list(range(n))
        while remaining:
            best = None
            bt = None
            for i in remaining[:48]:
                if any(not done[j] for j in deps[i]):
                    continue
                en = ops[i][0]
                t0 = free.get(en, 0.0)
                for j in deps[i]:
                    t0 = max(t0, end[j] + (hop if ops[j][0] != en else 0.05))
                if bt is None or t0 < bt - 1e-9:
                    bt = t0
                    best = i
            i = best
            en = ops[i][0]
            end[i] = bt + dur[i]
            free[en] = end[i]
            done[i] = True
            order.append(i)
            remaining.remove(i)
        sv = self.stage
        for i in order:
            en, fn, reads, writes, cost, stg = ops[i]
            self.stage = stg
            self.op(en, fn, reads=reads, writes=writes)
        self.stage = sv
        return max(end) if n else 0.0

    def tt(self, en, out, in0, in1, op, rd=None, wr=None):
        self.op(en, lambda h: h.tensor_tensor(out=out, in0=in0, in1=in1, op=op),
                reads=rd if rd is not None else [in0, in1], writes=wr if wr is not None else [out])

    def ts(self, en, out, in0, s1, s2, op0, op1=None, rd=None):
        def f(h):
            if op1 is None:
                return h.tensor_scalar(out=out, in0=in0, scalar1=s1, scalar2=None, op0=op0)
            return h.tensor_scalar(out=out, in0=in0, scalar1=s1, scalar2=s2, op0=op0, op1=op1)
        r = [in0] + [a for a in (s1, s2) if not isinstance(a, (int, float)) and a is not None]
        self.op(en, f, reads=rd if rd is not None else r, writes=[out])

    def act(self, out, in_, func, bias=None, scale=None, accum=None):
        kw = {}
        r = [in_]
        w = [out]
        if bias is not None:
            kw["bias"] = bias
            if not isinstance(bias, float):
                r.append(bias)
        if scale is not None:
            kw["scale"] = scale
            if not isinstance(scale, float):
                r.append(scale)
        if accum is not None:
            kw["accum_out"] = accum
            w.append(accum)
        self.op("act", lambda h: h.activation(out=out, in_=in_, func=func, **kw), reads=r, writes=w)

    def mm(self, out, pairs, rd=None, first=True, last=True):
        n = len(pairs)

        def f(h):
            ins = None
            for i, (l, r) in enumerate(pairs):
                ins = h.matmul(out, lhsT=l, rhs=r, start=(first and i == 0), stop=(last and i == n - 1))
            return ins
        r = rd if rd is not None else [a for p in pairs for a in p]
        nfree = 1
        for st, c in pairs[0][1].ap[1:]:
            nfree *= c
        c1 = max(nfree, 64) * 0.00052 * (4 if pairs[0][0].dtype == F32 else 1) + 0.03
        self.op("pe", f, reads=r, writes=[out], cost=n * c1)

    def tr(self, out, in_, ident):
        self.op("pe", lambda h: h.transpose(out=out, in_=in_, identity=ident), reads=[in_, ident], writes=[out])

    def trs(self, outs_ins, ident, rd, wr):
        def f(h):
            ins = None
            for o, i in outs_ins:
                ins = h.transpose(out=o, in_=i, identity=ident)
            return ins
        self.op("pe", f, reads=rd + [ident], writes=wr, cost=0.09 * len(outs_ins))

    def issue_load(self):
        if self.nload >= self.total_loads:
            return
        i = self.nload
        k = i % NRING
        pid = i % NPIECE
        self.dma("sp", self.ring[k], self.wsc_d[pid], self.ring_sem[k], reads=[self.conv_res[pid]], writes=[self.ring[k]])
        self.nload += 1

    def nextw(self):
        k = self.nuse % NRING
        self.nuse += 1
        return self.ring[k]

    def donew(self, n=1):
        for _ in range(n):
            self.nrel += 1
            self.issue_load()

    def tap(self, ap, col0, n):
        if not self.dbg_cols:
            return
        parts = ap.shape[0]
        self.dma("pool", self.dbg_d[0:parts, col0:col0 + n], ap, self.dbg_sem, reads=[ap], writes=[self.dbg_res])

    def build(self):
        nc = self.nc
        NB, NT = self.NB, self.NT
        self.total_loads = NB * NT * NPIECE
        self.dbg_sem = self.newsem("dbg")
        self.dbg_res = Res()
        self.x_tm = self.buf([NSUB, D], F32)
        self.cst = self.buf([NCST], F32)
        cmat = self.buf([NCM], F32)
        self.nfb = self.buf([D], F32)
        self.identb = self.buf([128], BF16)
        self.trib = self.buf([128], BF16)
        self.onesb = self.buf([128], BF16)
        self.rqb = self.buf([96], BF16)
        self.identf = cmat[:, M_ID:M_ID + 128]
        self.Uf = cmat[:, M_U:M_U + 128]
        self.SUf = cmat[:, M_SU:M_SU + 128]
        self.onesf = cmat[:, M_ONE:M_ONE + 128]
        self.mhalf = self.buf([512], F32)
        self.dg = self.buf([48, 128], BF16)
        self.dgD = self.buf([8, 128], BF16)
        self.modT = self.buf([72, NB], F32)
        self.cact = self.buf([KD, NB], F32)
        self.Abc = self.buf([16], F32)
        self.AS = self.buf([6, 8], F32)
        self.G = self.buf([3, D], F32)
        self.state = self.buf([D], F32)
        self.stateb = self.buf([D], BF16)
        self.carry = self.buf([12, 4], BF16)
        self.ss = self.buf([16], F32)
        self.vv = self.buf([16], F32)
        self.rstd = self.buf([16], F32)
        self.vv2 = self.buf([8], F32)
        self.rs2 = self.buf([8], F32)
        self.ring = [self.buf([PIECE], BF16) for _ in range(NRING)]
        self.ring_sem = [self.newsem("ring%d" % k) for k in range(NRING)]
        self.phase0 = self.off
        ld = self.newsem("ld0")
        self.dma("sp", self.cst, self.cst_d, ld, writes=[self.cst])
        self.dma("sp", cmat, self.cmat_d, ld, writes=[cmat])
        self.dma("sp", self.nfb, self.nfb_d, ld, writes=[self.nfb])
        cT = self.buf([KD, NB], F32)
        self.dma("sp", cT, self.cT_d.rearrange("p (k b) -> p k b", k=KD), ld, writes=[cT])
        self.conv_res = [Res() for _ in range(NPIECE)]
        self.conv_sem = [self.newsem("cv%d" % i) for i in range(NPIECE)]
        for i in range(NPIECE):
            self.dma("pool", self.wsc_d[i], self.wbig_d[i], self.conv_sem[i], writes=[self.conv_res[i]])
        for _ in range(NRING):
            self.issue_load()
        self.op("dve", lambda h: h.tensor_copy(out=self.identb, in_=self.identf), reads=[self.identf], writes=[self.identb])
        self.op("dve", lambda h: h.tensor_copy(out=self.trib, in_=self.Uf), reads=[self.Uf], writes=[self.trib])
        self.op("dve", lambda h: h.tensor_copy(out=self.onesb, in_=self.onesf), reads=[self.onesf], writes=[self.onesb])
        self.op("dve", lambda h: h.tensor_copy(out=self.rqb, in_=cmat[:, M_RQ:M_RQ + 96]), reads=[cmat[:, M_RQ:M_RQ + 96]], writes=[self.rqb])
        self.op("pool", lambda h: h.memset(self.mhalf, -0.5), writes=[self.mhalf])
        for j in range(4):
            for c in range(12):
                col = C_CONVW + j * 12 + c
                self.ts("dve", self.dg[:, j * 12 + c, :], self.identf, self.cst[:, col:col + 1], None, ALU.mult)
        for c in range(8):
            self.ts("dve", self.dgD[:, c, :], self.identf, self.cst[:, C_DFEAT + c:C_DFEAT + c + 1], None, ALU.mult)
        self.act(self.Abc, self.cst[:, C_ALOG:C_ALOG + 16], AF.Exp)
        self.ts("dve", self.Abc, self.Abc, -1.0, None, ALU.mult)
        self.act(self.cact, cT, AF.Silu)
        wa = [self.buf([KD, 512], F32) for _ in range(2)]
        wa_sem = [self.newsem("wa%d" % k) for k in range(2)]
        pmod = self.bank(0)[:, 0:72 * NB].rearrange("p (f b) -> p f b", b=NB)
        for g in range(18):
            w = wa[g % 2]
            self.dma("sp", w, self.wada_d[g].rearrange("p (k c) -> p k c", k=KD), wa_sem[g % 2], writes=[w])
            for cc in range(4):
                fc = g * 4 + cc
                self.mm(pmod[:, fc, :], [(w[:, kd, cc * 128:(cc + 1) * 128], self.cact[:, kd, :]) for kd in range(KD)])
        self.tt("dve", self.modT, pmod, self.cst[:, C_BADA:C_BADA + 72].unsqueeze(2).to_broadcast([128, 72, NB]), ALU.add)
        self.free(cT, *wa)
        self.out_sem = self.newsem("outst")
        self.pos_sem = self.newsem("posld")
        self.kst_sem = self.newsem("kvst")
        self.kb_sem = [self.newsem("kb%d" % i) for i in range(2)]
        self.vb_sem = [self.newsem("vb%d" % i) for i in range(2)]
        self.kv_res = [Res() for _ in range(NB)]
        self.x_sem = self.newsem("xld")
        self.out_res = Res()
        self.x_sems = [self.newsem("xld%d" % i) for i in range(NSUB)]
        order = [(b, t) for b in range(NB) for t in range(NT)]
        for s_ in range(NSUB):
            self.load_x(0, 0, s_)
        for i, (b, t) in enumerate(order):
            if t == 0:
                self.batch_setup(b)
            self.tile(b, t, order[i + 1] if i + 1 < len(order) else None)
        h, own, waited = self.eng["sp"]
        h.wait_ge(self.out_sem.h, self.out_sem.cnt)
        if self.dbg_cols:
            h.wait_ge(self.dbg_sem.h, self.dbg_sem.cnt)
        return nc

    def mod(self, m, b):
        return self.modT[:, m * 8:(m + 1) * 8, b]

    def batch_setup(self, b):
        self.stage = "bsetup"
        t8 = self.buf([8], F32)
        for k in range(3):
            self.ts("dve", t8, self.mod(3 * k + 1, b), 1.0, None, ALU.add)
            self.tt("dve", self.AS[:, 2 * k, :], t8, self.cst[:, C_NF + 8 * k:C_NF + 8 * k + 8], ALU.mult)
            self.op("dve", lambda h, k=k: h.tensor_copy(out=self.AS[:, 2 * k + 1, :], in_=self.mod(3 * k, b)),
                    reads=[self.modT], writes=[self.AS[:, 2 * k + 1, :]])
        gb = [self.buf([128], F32) for _ in range(2)]
        n = 0
        for k in range(3):
            for kd in range(KD):
                g = gb[n % 2]
                pb = self.bank(n % 2)[:, 0:128]
                n += 1
                sc = 1.0 if k == 1 else 0.5
                col = self.modT[:, (3 * k + 2) * 8 + kd, b:b + 1]
                self.ts("dve", g, self.onesf, col, sc, ALU.mult, ALU.mult)
                self.mm(pb, [(g, self.identf)])
                self.op("act", lambda h, k=k, kd=kd, pb=pb: h.copy(out=self.G[:, k, kd * 128:(kd + 1) * 128], in_=pb),
                        reads=[pb], writes=[self.G[:, k, kd * 128:(kd + 1) * 128]])
        self.free(t8, *gb)
        self.op("pool", lambda h: h.memset(self.state, 0.0), writes=[self.state])
        self.op("pool", lambda h: h.memset(self.stateb, 0.0), writes=[self.stateb])
        self.op("pool", lambda h: h.memset(self.carry, 0.0), writes=[self.carry])

    def norm_begin(self, k):
        st = dict(k=k, xn=[self.buf([D], BF16) for _ in range(2)], tmp=[self.buf([8, 128], F32) for _ in range(2)],
                  junk=self.buf([D], BF16))
        return st

    def norm_end(self, st):
        self.free(*st["xn"], *st["tmp"], st["junk"])

    def norm_front(self, st, s):
        k = st["k"]
        sv = self.stage
        self.stage = "norm%d" % k
        c = 4 * k + s
        xs = self.x_tm[:, s, :]
        x_n = st["xn"][s % 2]
        self.act(st["junk"], xs, AF.Square, accum=self.ss[:, c:c + 1])
        self.act(self.vv[:, c:c + 1], self.ss[:, c:c + 1], AF.Ln, bias=EPS, scale=1.0 / D)
        self.act(self.rstd[:, c:c + 1], self.vv[:, c:c + 1], AF.Exp, scale=-0.5)
        self.ts("dve", x_n, xs, self.rstd[:, c:c + 1], None, ALU.mult)
        self.stage = sv

    def norm_pe(self, st, s):
        sv = self.stage
        self.stage = "norm%d" % st["k"]
        x_n = st["xn"][s % 2]
        pb = self.bank(s, BF16).rearrange("p (a b) -> p a b", a=8)
        self.trs([(pb[:, kd, :], x_n[:, kd * 128:(kd + 1) * 128]) for kd in range(KD)], self.identb, [x_n], [pb])
        self.stage = sv

    def norm_back(self, st, s, hT):
        k = st["k"]
        A = self.AS[:, 2 * k, :].unsqueeze(2).to_broadcast([128, 8, 128])
        SH = self.AS[:, 2 * k + 1, :].unsqueeze(2).to_broadcast([128, 8, 128])
        pb = self.bank(s, BF16).rearrange("p (a b) -> p a b", a=8)
        tmp = st["tmp"][s % 2]
        self.tt("dve", tmp, pb, A, ALU.mult, rd=[pb, self.AS])
        self.tt("dve", hT[:, :, s * 128:(s + 1) * 128], tmp, SH, ALU.add, rd=[tmp, self.AS])

    def norm_mod(self, k, hT):
        st = self.norm_begin(k)
        for i in range(NSUB + 1):
            if i < NSUB:
                self.norm_front(st, i)
                self.norm_pe(st, i)
            if i >= 1:
                self.norm_back(st, i - 1, hT)
        self.norm_end(st)

    def tail_norm(self, k, hT):
        st = self.norm_begin(k)

        def after_evac(s):
            self.norm_front(st, s)

        def pe_slot(s):
            if s >= 1:
                self.norm_pe(st, s - 1)
                self.norm_back(st, s - 1, hT)

        def finish():
            self.norm_pe(st, NSUB - 1)
            self.norm_back(st, NSUB - 1, hT)
            self.norm_end(st)
        return after_evac, pe_slot, finish

    def tail_final(self, b, t, nxt):
        o = self.buf([NSUB, D], F32)
        junk = self.buf([D], BF16)

        def after_evac(s):
            sv = self.stage
            self.stage = "final"
            xs = self.x_tm[:, s, :]
            c = 12 + s
            self.act(junk, xs, AF.Square, accum=self.ss[:, c:c + 1])
            self.act(self.vv[:, c:c + 1], self.ss[:, c:c + 1], AF.Ln, bias=EPS, scale=1.0 / D)
            self.act(self.rstd[:, c:c + 1], self.vv[:, c:c + 1], AF.Exp, scale=-0.5)
            self.ts("dve", o[:, s, :], xs, self.rstd[:, c:c + 1], None, ALU.mult)
            self.tt("dve", o[:, s, :], o[:, s, :], self.nfb, ALU.mult)
            xo = self.out_d[b, t * T + s * 128:t * T + (s + 1) * 128, :]
            self.dma("pool", xo, o[:, s, :], self.out_sem, reads=[o[:, s, :]], writes=[self.out_res])
            if nxt is not None:
                self.load_x(nxt[0], nxt[1], s)
            self.stage = sv

        def pe_slot(s):
            pass

        def finish():
            self.free(o, junk)
        return after_evac, pe_slot, finish

    def load_x(self, b, t, s):
        xin = self.x_d[b, t * T + s * 128:t * T + (s + 1) * 128, :]
        self.dma("sp", self.x_tm[:, s, :], xin, self.x_sems[s], writes=[self.x_tm[:, s, :]])

    def ffn(self, hT, gk, tail, side=()):
        self.stage = "ffn_gu"
        actT = self.buf([NHC, T], BF16)
        sgt = [self.buf([T], F32) for _ in range(2)]
        tmp = [self.buf([T], F32) for _ in range(2)]
        for hg in range(NHG):
            w = self.nextw()
            wv = w.rearrange("p (g k c) -> p g k c", g=2, k=KD)
            for c in range(2):
                hc = hg * 2 + c
                pg = self.bank(2 * (hc % 2))
                pu = self.bank(2 * (hc % 2) + 1)
                if hg == 0:
                    for (ca, cb) in ((0, 384), (384, 512)):
                        self.mm(pg[:, ca:cb], [(wv[:, 0, kd, c * 128:(c + 1) * 128], hT[:, kd, ca:cb]) for kd in range(KD)],
                                rd=[w, hT[:, :, ca:cb]])
                        self.mm(pu[:, ca:cb], [(wv[:, 1, kd, c * 128:(c + 1) * 128], hT[:, kd, ca:cb]) for kd in range(KD)],
                                rd=[w, hT[:, :, ca:cb]])
                else:
                    self.mm(pg, [(wv[:, 0, kd, c * 128:(c + 1) * 128], hT[:, kd, :]) for kd in range(KD)], rd=[w, hT])
                    self.mm(pu, [(wv[:, 1, kd, c * 128:(c + 1) * 128], hT[:, kd, :]) for kd in range(KD)], rd=[w, hT])
                self.act(sgt[hc % 2], pg, AF.Silu)
                self.tt("dve", actT[:, hc, :], sgt[hc % 2], pu, ALU.mult)
                if hc < len(side):
                    sv = self.stage
                    self.stage = "rope"
                    side[hc]()
                    self.stage = sv
            self.donew()
        wd = [self.nextw() for _ in range(6)]
        self.stage = "ffn_down"
        n = 0
        after_evac, pe_slot, finish = tail
        for s in range(NSUB):
            pa = [self.bank(4 + 2 * (s % 2)), self.bank(5 + 2 * (s % 2))]

            def f(h, s=s, pa=pa):
                ins = None
                for hc in range(NHC):
                    wdv = wd[hc // 4][:, (hc % 4) * D:(hc % 4 + 1) * D]
                    for half in range(2):
                        ins = h.matmul(pa[half], lhsT=actT[:, hc, s * 128:(s + 1) * 128],
                                       rhs=wdv[:, half * 512:(half + 1) * 512], start=(hc == 0), stop=(hc == NHC - 1))
                return ins
            self.op("pe", f, reads=[actT[:, :, s * 128:(s + 1) * 128]] + wd, writes=pa)
            pe_slot(s)
            for half in range(2):
                tp = tmp[n % 2]
                n += 1
                xs = self.x_tm[:, s, half * 512:(half + 1) * 512]
                self.tt("dve", tp, pa[half], self.G[:, gk, half * 512:(half + 1) * 512], ALU.mult)
                self.tt("dve", xs, xs, tp, ALU.add)
            after_evac(s)
        self.donew(6)
        self.free(actT, *sgt, *tmp)
        finish()

    def skip_pieces(self, n):
        for _ in range(n):
            self.nextw()
            self.donew()

    def tile(self, b, t, nxt):
        hT = self.buf([KD, T], BF16)
        self.norm_mod(0, hT)
        if self.stop == "nomix":
            self.ffn(hT, 0, self.tail_norm(2, hT))
            self.skip_pieces(14)
        else:
            side = self.rope_side_ops(b, t)
            self.ffn(hT, 0, self.tail_norm(1, hT), side=side)
            hT = self.mixer(b, t, hT)
        self.ffn(hT, 2, self.tail_final(b, t, nxt))
        self.free(hT)
        self.live = {}

    def rstd_of(self, ssq, n, inv_n):
        self.ts("dve", self.vv2[:, 0:n], ssq, inv_n, EPS, ALU.mult, ALU.add)
        self.tt("pool", self.rs2[:, 0:n], self.vv2[:, 0:n], self.mhalf[:, 0:n], ALU.pow)
        return self.rs2

    def rope_side_ops(self, b, t):
        PI = math.pi
        c0 = t * T
        self.cs = cs = self.buf([T], F32)
        posi = self.buf([T], I32)
        a0 = self.buf([T], F32)
        kf = self.buf([T], F32)
        self.rope_tmp = [posi, a0, kf]
        C1 = 6.28125
        C2 = 2 * PI - C1
        stt = lambda out, in0, sc, in1, o0, o1: self.op(
            "dve", lambda h: h.scalar_tensor_tensor(out=out, in0=in0, scalar=sc, in1=in1, op0=o0, op1=o1),
            reads=[in0, in1], writes=[out])
        ops = [
            lambda: self.dma("sp", posi, self.pos_d[b:b + 1, c0:c0 + T].partition_broadcast(128), self.pos_sem, writes=[posi]),
            lambda: self.op("dve", lambda h: h.tensor_copy(out=a0, in_=posi), reads=[posi], writes=[a0]),
            lambda: self.ts("dve", a0, a0, self.cst[:, C_INVF:C_INVF + 1], None, ALU.mult),
            lambda: self.ts("dve", kf, a0, 1.0 / (2 * PI), None, ALU.mult),
            lambda: self.op("dve", lambda h: h.tensor_copy(out=posi, in_=kf), reads=[kf], writes=[posi]),
            lambda: self.op("dve", lambda h: h.tensor_copy(out=kf, in_=posi), reads=[posi], writes=[kf]),
            lambda: stt(a0, kf, -C1, a0, ALU.mult, ALU.add),
            lambda: stt(a0, kf, -C2, a0, ALU.mult, ALU.add),
            lambda: self.ts("dve", a0, a0, self.cst[:, C_PHASE:C_PHASE + 1], None, ALU.add),
        ]
        for _ in range(2):
            ops += [
                lambda: self.ts("dve", kf, a0, PI, 2 * PI, ALU.is_gt, ALU.mult),
                lambda: self.tt("dve", a0, a0, kf, ALU.subtract),
                lambda: self.ts("dve", kf, a0, -PI, 2 * PI, ALU.is_lt, ALU.mult),
                lambda: self.tt("dve", a0, a0, kf, ALU.add),
            ]
        ops += [
            lambda: self.ts("dve", a0, a0, PI, -PI, ALU.min, ALU.max),
            lambda: None,
            lambda: self.act(cs, a0, AF.Sin),
        ]
        return ops

    def mixer(self, b, t, hT):
        S = self.S
        PI = math.pi
        rot = [0]

        def rb():
            rot[0] = (rot[0] + 1) % 4
            return self.bank(rot[0])
        c0 = t * T
        self.stage = "m1_z"
        zt = self.buf([NSUB, D], BF16)
        xpre = self.buf([12, T + 4], BF16)
        dt_tm = self.buf([NSUB, 16], F32)
        cqT = self.buf([5, T], BF16)
        cs = self.cs
        tmpk = self.buf([T], BF16)
        for half in range(2):
            w = self.nextw().rearrange("p (k c) -> p k c", k=KD)
            for s in range(NSUB):
                pb = rb()
                self.mm(pb, [(hT[:, kd, s * 128:(s + 1) * 128], w[:, kd, :]) for kd in range(KD)])
                self.act(zt[:, s, half * 512:(half + 1) * 512], pb, AF.Silu)
            self.donew()
        self.stage = "m1_tok"
        wa = self.nextw()[:, 0:KD * 400].rearrange("p (k c) -> p k c", k=KD)
        wb = self.nextw()[:, 0:KD * 256].rearrange("p (k c) -> p k c", k=KD)
        cqn = [self.buf([640], BF16) for _ in range(4)]
        junk = self.buf([384], BF16)
        t16 = self.buf([NSUB, 16], F32)
        ssq = self.buf([8], F32)
        for s in range(NSUB):
            pa = self.bank(4 + s % 2)[:, 0:400]
            pk = self.bank(6 + s % 2)[:, 0:256]
            hs = [hT[:, kd, s * 128:(s + 1) * 128] for kd in range(KD)]
            self.mm(pa, [(hs[kd], wa[:, kd, :]) for kd in range(KD)])
            self.mm(pk, [(hs[kd], wb[:, kd, :]) for kd in range(KD)])
            self.act(junk, pa[:, 0:384], AF.Square, accum=ssq[:, 2 * s:2 * s + 1])
            self.act(junk[:, 0:256], pk, AF.Square, accum=ssq[:, 2 * s + 1:2 * s + 2])
            self.act(self.vv2[:, 2 * s:2 * s + 1], ssq[:, 2 * s:2 * s + 1], AF.Ln, bias=EPS, scale=1.0 / 384)
            self.act(self.vv2[:, 2 * s + 1:2 * s + 2], ssq[:, 2 * s + 1:2 * s + 2], AF.Ln, bias=EPS, scale=1.0 / 256)
            self.act(self.rs2[:, 2 * s:2 * s + 2], self.vv2[:, 2 * s:2 * s + 2], AF.Exp, scale=-0.5)
            cq = cqn[s]
            self.tt("dve", t16[:, s, :], pa[:, 384:400], self.cst[:, C_DTB:C_DTB + 16], ALU.add)
            self.act(cq[:, 0:384], pa[:, 0:384], AF.Copy, scale=self.rs2[:, 2 * s:2 * s + 1])
            self.act(cq[:, 384:640], pk, AF.Copy, scale=self.rs2[:, 2 * s + 1:2 * s + 2])
        self.donew(2)
        self.act(t16, t16, AF.Exp)
        self.act(dt_tm, t16, AF.Ln, bias=1.0)

        def tok_tr(s):
            sv = self.stage
            self.stage = "m1_tok"
            cq = cqn[s]
            pTb = self.bank(4 + s % 2, BF16)[:, 0:640].rearrange("p (a c) -> p a c", a=5)
            self.trs([(pTb[:, i, :], cq[:, i * 128:(i + 1) * 128]) for i in range(5)], self.identb, [cq], [pTb])
            self.tt("dve", cqT[:, :, s * 128:(s + 1) * 128], pTb,
                    self.cst[:, C_QKVN:C_QKVN + 5].unsqueeze(2).to_broadcast([128, 5, 128]), ALU.mult)
            self.stage = sv
        self.stage = "m1_x"
        self.op("pool", lambda h: h.tensor_copy(out=xpre[:, :, 1:4], in_=self.carry[:, :, 0:3]), reads=[self.carry], writes=[xpre[:, :, 0:4]])
        for g in range(3):
            w = self.nextw().rearrange("p (k c) -> p k c", k=KD)
            for cc in range(4):
                c = 4 * g + cc
                pb = rb()
                self.mm(pb, [(w[:, kd, cc * 128:(cc + 1) * 128], hT[:, kd, :]) for kd in range(KD)])
                if c % 2 == 0:
                    self.op("act", lambda h, pb=pb, c=c: h.copy(out=xpre[:, c, 4:4 + T], in_=pb), reads=[pb], writes=[xpre[:, c, 4:4 + T]])
                else:
                    self.op("dve", lambda h, pb=pb, c=c: h.tensor_copy(out=xpre[:, c, 4:4 + T], in_=pb), reads=[pb], writes=[xpre[:, c, 4:4 + T]])
                if g < 2 and cc % 2 == 1:
                    tok_tr(2 * g + cc // 2)
            self.donew()
        self.free(junk, t16, ssq, *cqn)
        w = self.nextw()[:, 0:KD * 64].rearrange("p (k c) -> p k c", k=KD)
        pkr = rb()[0:64, :]
        self.mm(pkr, [(w[:, kd, :], hT[:, kd, :]) for kd in range(KD)])
        self.tt("dve", tmpk[0:64, :], pkr, cs[0:64, :], ALU.mult)
        self.donew()
        self.free(hT)
        if self.stop == "m1":
            return self.bail(6)
        self.stage = "m2_conv"
        xbc = self.buf([12, T], BF16)
        for c in range(12):
            pb = rb()
            self.mm(pb, [(self.dg[:, j * 12 + c, :], xpre[:, c, 1 + j:1 + j + T]) for j in range(4)])
            self.act(xbc[:, c, :], pb, AF.Silu, bias=self.cst[:, C_CONVB + c:C_CONVB + c + 1])
        self.op("pool", lambda h: h.tensor_copy(out=self.carry[:, :, 0:3], in_=xpre[:, :, T + 1:T + 4]), reads=[xpre], writes=[self.carry])
        self.free(xpre)
        if self.stop == "m2":
            return self.bail(6)
        self.stage = "m3_qkv"
        QT = self.buf([8, T], BF16)
        KT = self.buf([8, T], BF16)
        Vc = self.buf([8, NSUB, 128], BF16)
        tmpq = [self.buf([T], BF16) for _ in range(2)]
        wq = self.nextw()[:, 0:3072].rearrange("p (k c) -> p k c", k=3)
        wkv = self.nextw()[:, 0:3584].rearrange("p (k c) -> p k c", k=2)
        def q_a(hh):
            pq = self.bank(hh % 2)
            self.mm(pq, [(wq[:, kc, hh * 128:(hh + 1) * 128], cqT[:, kc, :]) for kc in range(3)])
            self.tt("dve", tmpq[hh % 2], pq, cs, ALU.mult)

        def k_a(hh):
            pk = self.bank(4 + hh % 2)

            def fk(h):
                h.matmul(pk[0:96, :], lhsT=self.rqb[0:64, :], rhs=tmpk[0:64, :], start=True, stop=False)
                h.matmul(pk[0:96, :], lhsT=wkv[:, 0, hh * 96:(hh + 1) * 96], rhs=cqT[:, 3, :], start=False, stop=False)
                return h.matmul(pk[0:96, :], lhsT=wkv[:, 1, hh * 96:(hh + 1) * 96], rhs=cqT[:, 4, :], start=False, stop=True)
            self.op("pe", fk, reads=[self.rqb, tmpk, wkv, cqT], writes=[pk])
            self.op("dve", lambda h: h.tensor_copy(out=KT[0:96, hh, :], in_=pk[0:96, :]), reads=[pk], writes=[KT[:, hh, :]])

        def q_b(hh):
            pq2 = self.bank(2 + hh % 2)[0:96, :]
            self.mm(pq2, [(self.rqb, tmpq[hh % 2])])
            self.act(QT[0:96, hh, :], pq2, AF.Copy, scale=QSCALE)
        for hh in range(9):
            if hh < 8:
                q_a(hh)
                k_a(hh)
            if hh >= 1:
                q_b(hh - 1)
        for s in range(NSUB):
            for half in range(2):
                pv = self.bank(6 + half)
                self.mm(pv, [(cqT[:, 3 + kc, s * 128:(s + 1) * 128], wkv[:, kc, 768 + half * 512:768 + (half + 1) * 512]) for kc in range(2)])
                self.op("act", lambda h, pv=pv, s=s, half=half: h.copy(out=Vc[:, half * 4:(half + 1) * 4, s, :], in_=pv.rearrange("p (a v) -> p a v", a=4)),
                        reads=[pv], writes=[Vc])
        self.donew(2)
        kres = self.kv_res[b]
        self.dma("pool", self.kc_d[b].rearrange("h r s -> r h s")[:, :, c0:c0 + T], KT[0:96, :, :], self.kst_sem, reads=[KT], writes=[kres])
        self.dma("pool", self.vc_d[b].rearrange("h p n -> p h n")[:, :, c0:c0 + T], Vc.rearrange("p h s v -> p h (s v)"), self.kst_sem, reads=[Vc], writes=[kres])
        self.free(KT, Vc, cs, tmpk, cqT, *tmpq, *self.rope_tmp)
        if self.stop == "m3":
            return self.bail(4)
        ycat = self.buf([8, T], BF16)
        self.ssd(zt, xbc, dt_tm, ycat)
        self.free(zt, xbc, dt_tm)
        if self.stop == "ssd":
            return self.bail(4)
        ycm = self.buf([8, T], BF16)
        self.attention(b, t, QT, ycm)
        self.free(QT)
        if self.stop == "attn":
            return self.bail(4)
        self.stage = "m7_wout"
        wo = [self.nextw().rearrange("p (j f) -> p j f", j=4) for _ in range(4)]
        tmp = [self.buf([T], F32) for _ in range(2)]
        n = 0
        hT2 = self.buf([KD, T], BF16)
        after_evac, pe_slot, finish = self.tail_norm(2, hT2)
        for s in range(NSUB):
            pa = [self.bank(4 + 2 * (s % 2)), self.bank(5 + 2 * (s % 2))]

            for part, yc in ((0, ycat), (1, ycm)):
                def f(h, s=s, pa=pa, part=part, yc=yc):
                    ins = None
                    for kc in range(8 * part, 8 * part + 8):
                        for half in range(2):
                            ins = h.matmul(pa[half], lhsT=yc[:, kc % 8, s * 128:(s + 1) * 128],
                                           rhs=wo[kc // 4][:, kc % 4, half * 512:(half + 1) * 512], start=(kc == 0), stop=(kc == 15))
                    return ins
                self.op("pe", f, reads=[yc[:, :, s * 128:(s + 1) * 128]] + wo[2 * part:2 * part + 2], writes=pa)
            pe_slot(s)
            for half in range(2):
                tp = tmp[n % 2]
                n += 1
                xs = self.x_tm[:, s, half * 512:(half + 1) * 512]
                self.tt("dve", tp, pa[half], self.G[:, 1, half * 512:(half + 1) * 512], ALU.mult)
                self.tt("dve", xs, xs, tp, ALU.add)
            after_evac(s)
        self.donew(4)
        self.free(ycat, ycm, *tmp)
        finish()
        return hT2

    def bail(self, npieces):
        self.skip_pieces(npieces)
        self.live = {}
        hT2 = self.buf([KD, T], BF16)
        self.norm_mod(2, hT2)
        return hT2

    def ssd(self, zt, xbc, dt_tm, ycat):
        self.stage = "ssd"
        R = self.buf([16, 128], F32)
        Lx = [self.buf([16, 128], BF16) for _ in range(2)]
        CBm = [self.buf([2, 128], BF16) for _ in range(2)]
        xdt = [self.buf([D], BF16) for _ in range(2)]
        xdtd = [self.buf([D], BF16) for _ in range(2)]
        Btm = [self.buf([2, 128], BF16) for _ in range(2)]
        yv = self.buf([D], F32)
        ygn = self.buf([D], BF16)
        small = self.buf([10, 16], F32)
        dA = [small[:, 0, :], small[:, 1, :]]
        acs = small[:, 2, :]
        ea = [small[:, 3, :], small[:, 4, :]]
        ds = [small[:, 5, :], small[:, 6, :]]
        cd = [small[:, 7, :], small[:, 8, :]]
        ssq = small[:, 9, 0:4]
        junk = self.buf([512], BF16)
        b16 = lambda a: a.unsqueeze(2).to_broadcast([128, 16, 64])
        v16 = lambda a: a.rearrange("p (h x) -> p h x", h=16)
        Dbc = self.cst[:, C_DSK:C_DSK + 16]
        R2 = R.rearrange("p h l -> p (h l)")
        bk2 = self.bank(2)
        pacc = bk2[:, 0:16]
        ptot = bk2[:, 16:32]
        pcb = bk2[:, 32:288].rearrange("p (g l) -> p g l", g=2)
        pbt = bk2[:, 288:416].bitcast(BF16).rearrange("p (g n) -> p g n", g=2)
        pxs = self.bank(3, BF16)

        def front(c):
            k = c % 2
            cols = slice(c * 128, (c + 1) * 128)
            Lk = Lx[k]
            Lk2 = Lk.rearrange("p h l -> p (h l)")
            Lk4 = Lk.rearrange("p (g r) l -> p g r l", g=2)

            def s0():
                self.tt("dve", dA[k], dt_tm[:, c, :], self.Abc, ALU.mult)
                self.trs([(pxs[:, i * 128:(i + 1) * 128], xbc[:, i, cols]) for i in range(8)], self.identb, [xbc[:, 0:8, cols]], [pxs])
                self.trs([(pbt[:, g, :], xbc[:, 8 + g, cols]) for g in range(2)], self.identb, [xbc[:, 8:10, cols]], [pbt])
                for g in range(2):
                    self.mm(pcb[:, g, :], [(xbc[:, 8 + g, cols], xbc[:, 10 + g, cols])])

            def s1():
                self.tt("dve", R, self.Uf.unsqueeze(1).to_broadcast([128, 16, 128]), dA[k].unsqueeze(2).to_broadcast([128, 16, 128]), ALU.mult)
                self.mm(pacc, [(self.Uf, dA[k])])
                self.mm(ptot, [(self.onesf, dA[k])])
                self.tt("dve", v16(xdt[k]), v16(pxs), b16(dt_tm[:, c, :]), ALU.mult)
                self.op("act", lambda h: h.copy(out=Btm[k], in_=pbt), reads=[pbt], writes=[Btm[k]])
                self.tt("dve", CBm[k], pcb, self.trib.unsqueeze(1).to_broadcast([128, 2, 128]), ALU.mult)

            def s2():
                for q in range(2):
                    self.mm(self.bank(q), [(self.SUf, R2[:, q * 512:(q + 1) * 512])])
                self.op("dve", lambda h: h.tensor_copy(out=acs, in_=pacc), reads=[pacc], writes=[acs])
                self.act(ea[k], pacc, AF.Exp)
                self.act(cd[k], ptot, AF.Exp)
                self.tt("dve", ds[k], ptot, acs, ALU.subtract)
                self.act(ds[k], ds[k], AF.Exp)

            def s3():
                for q in range(2):
                    self.act(Lk2[:, q * 512:(q + 1) * 512], self.bank(q), AF.Exp)
                self.tt("dve", v16(xdtd[k]), v16(xdt[k]), b16(ds[k]), ALU.mult)

            def s4():
                for q in range(2):
                    self.mm(self.bank(q), [(self.SUf, R2[:, 1024 + q * 512:1024 + (q + 1) * 512])])
                self.tt("dve", Lk4[:, 0, :, :], Lk4[:, 0, :, :], CBm[k][:, 0, :].unsqueeze(1).to_broadcast([128, 8, 128]), ALU.mult)

            def s5():
                for q in range(2):
                    self.act(Lk2[:, 1024 + q * 512:1024 + (q + 1) * 512], self.bank(q), AF.Exp)

            def s6():
                self.tt("dve", Lk4[:, 1, :, :], Lk4[:, 1, :, :], CBm[k][:, 1, :].unsqueeze(1).to_broadcast([128, 8, 128]), ALU.mult)
            return [s0, s1, s2, s3, s4, s5, s6]

        def back(c):
            k = c % 2
            cols = slice(c * 128, (c + 1) * 128)
            Lk = Lx[k]
            sq = ssq[:, 2 * k:2 * k + 2]
            vvk = self.vv2[:, 2 * k:2 * k + 2]
            rsk = self.rs2[:, 2 * k:2 * k + 2]
            pyo = self.pst[:, 2048:3072]
            pyd = self.pst[:, 3072:4096]

            def b0():
                for g in range(2):
                    self.mm(self.bank(4 + g), [(xbc[:, 10 + g, cols], self.stateb[:, g * 512:(g + 1) * 512])])
                for g in range(2):
                    pb = self.bank(6 + g)

                    def fy(h, g=g, pb=pb):
                        for i in range(4):
                            cc = 4 * g + i
                            h.matmul(pb[:, i * 128:(i + 1) * 128], lhsT=xbc[:, cc, cols], rhs=self.dgD[:, cc, :], start=(i == 0), stop=False)
                        ins = None
                        for r in range(8):
                            hh = g * 8 + r
                            ins = h.matmul(pb[:, r * 64:(r + 1) * 64], lhsT=Lk[:, hh, :], rhs=xdt[k][:, hh * 64:(hh + 1) * 64], start=False, stop=(r == 7))
                        return ins
                    self.op("pe", fy, reads=[self.dgD, xbc[:, 4 * g:4 * g + 4, cols], Lk, xdt[k]], writes=[pb], cost=0.9)

            def b1():
                self.tt("dve", v16(yv), v16(pyo), b16(ea[k]), ALU.mult)

            def b2():
                self.tt("dve", yv, yv, pyd, ALU.add)
                for g in range(2):
                    self.mm(self.bank(4 + g), [(Btm[k][:, g, :], xdtd[k][:, g * 512:(g + 1) * 512])])

            def b3():
                self.tt("dve", yv, yv, zt[:, c, :], ALU.mult)
                self.tt("dve", v16(self.state), v16(self.state), b16(cd[k]), ALU.mult)

            def b4():
                for g in range(2):
                    self.act(junk, yv[:, g * 512:(g + 1) * 512], AF.Square, accum=sq[:, g:g + 1])
                self.tt("dve", self.state, self.state, pyo, ALU.add)

            def b5():
                self.act(vvk, sq, AF.Ln, bias=EPS, scale=1.0 / 512)
                self.act(rsk, vvk, AF.Exp, scale=-0.5)

            def b6():
                self.op("act", lambda h: h.copy(out=self.stateb, in_=self.state), reads=[self.state], writes=[self.stateb])

            def b7():
                for g in range(2):
                    self.act(ygn[:, g * 512:(g + 1) * 512], yv[:, g * 512:(g + 1) * 512], AF.Copy, scale=rsk[:, g:g + 1])

            def b8():
                pT = self.bank(6, BF16).rearrange("p (a c) -> p a c", a=8)
                self.trs([(pT[:, i, :], ygn[:, i * 128:(i + 1) * 128]) for i in range(8)], self.identb, [ygn], [pT])

            def b9():
                pT = self.bank(6, BF16).rearrange("p (a c) -> p a c", a=8)
                self.tt("dve", ycat[:, 0:8, cols], pT, self.cst[:, C_SSDW:C_SSDW + 8].unsqueeze(2).to_broadcast([128, 8, 128]), ALU.mult)
            return [b0, b1, b2, b3, b4, b5, b6, b7, b8, b9]

        self.rec_begin()
        for c in range(NSUB):
            for st in front(c):
                st()
            for st in back(c):
                st()
        self.ssd_est = self.rec_end()
        self.free(R, *Lx, *CBm, *xdt, *xdtd, *Btm, yv, ygn, small, junk)

    def attention(self, b, t, QT, ycat):
        self.stage = "attn"
        S = self.S
        nk = 4 * (t + 1)
        kmax = nk * 128
        Kb = [self.buf([S], BF16) for _ in range(2)]
        Vb = [self.buf([S], BF16) for _ in range(2)]
        PT = [self.buf([T], BF16) for _ in range(3)]
        attnT = self.buf([8, T], BF16)
        rls = [self.buf([T], F32) for _ in range(2)]
        sqbs = [self.buf([T], BF16) for _ in range(2)]
        kres = self.kv_res[b]
        SSb = self.bank(3)
        steps = [(hh, j) for hh in range(8) for j in range(nk)]
        N = len(steps)
        LA = 2
        deferred = []

        def loadkv(hh):
            kb = Kb[hh % 2]
            vb = Vb[hh % 2]
            self.dma("sp", kb[0:96, 0:kmax], self.kc_d[b, hh, :, 0:kmax], self.kb_sem[hh % 2], reads=[kres], writes=[kb[:, 0:kmax]])
            self.dma("sp", vb[:, 0:kmax], self.vc_d[b, hh, :, 0:kmax], self.vb_sem[hh % 2], reads=[kres], writes=[vb[:, 0:kmax]])

        def geo(n):
            hh, j = steps[n]
            q0 = 128 * max(j - 4 * t, 0)
            return hh, j, q0, self.bank(n % 3)[:, q0:T], PT[n % 3]

        def emitS(n):
            hh, j, q0, ps, pt = geo(n)
            self.mm(ps, [(Kb[hh % 2][0:96, j * 128:(j + 1) * 128], QT[0:96, hh, q0:T])])

        def emitR(n):
            hh, j, q0, ps, pt = geo(n)
            vb = Vb[hh % 2]
            O = self.bank(4 + hh % 2)
            L = self.bank(6 + hh % 2)
            self.act(pt[:, q0:T], ps, AF.Exp)
            if j - 4 * t >= 0:
                self.tt("dve", pt[:, q0:q0 + 128], pt[:, q0:q0 + 128], self.trib, ALU.mult)

            def fo(h):
                h.matmul(O[:, q0:T], lhsT=vb[:, j * 128:(j + 1) * 128], rhs=pt[:, q0:T], start=(j == 0), stop=(j == nk - 1))
                return h.matmul(L[:, q0:T], lhsT=self.onesb, rhs=pt[:, q0:T], start=(j == 0), stop=(j == nk - 1))
            self.op("pe", fo, reads=[vb[:, j * 128:(j + 1) * 128], pt[:, q0:T], self.onesb], writes=[O[:, q0:T], L[:, q0:T]])
            if j == nk - 1:
                rl = rls[hh % 2]
                sqb = sqbs[hh % 2]
                self.op("dve", lambda h: h.reciprocal(out=rl, in_=L), reads=[L], writes=[rl])
                self.tt("dve", attnT[:, hh, :], O, rl, ALU.mult)
                deferred.append((n + 12, lambda: self.act(sqb, attnT[:, hh, :], AF.Square)))
                deferred.append((n + 16, lambda: self.mm(SSb, [(self.onesb, sqb)], first=(hh == 0), last=(hh == 7))))
                if hh + 2 < 8:
                    loadkv(hh + 2)

        loadkv(0)
        loadkv(1)
        for n in range(N + LA):
            if n < N:
                emitS(n)
            if n >= LA:
                emitR(n - LA)
            while deferred and deferred[0][0] <= n:
                deferred.pop(0)[1]()
            deferred.sort(key=lambda d: d[0])
        while deferred:
            deferred.pop(0)[1]()
        vvT = self.buf([T], F32)
        self.ts("dve", vvT, SSb, 1.0 / D, EPS, ALU.mult, ALU.add)
        self.act(vvT, vvT, AF.Ln)
        self.act(vvT, vvT, AF.Exp, scale=-0.5)
        for hh in range(8):
            self.op("dve", lambda h, hh=hh: h.scalar_tensor_tensor(out=ycat[:, hh, :], in0=attnT[:, hh, :], scalar=self.cst[:, C_MLAW + hh:C_MLAW + hh + 1],
                                                               in1=vvT, op0=ALU.mult, op1=ALU.mult),
                    reads=[attnT[:, hh, :], vvT, self.cst], writes=[ycat[:, hh, :]])
        self.free(vvT, attnT, *rls, *sqbs, *Kb, *Vb, *PT)

    def store_x(self, b, t):
        xo = self.out_d[b, t * T:(t + 1) * T, :].rearrange("(s p) d -> p s d", p=128)
        self.dma("pool", xo, self.x_tm, self.out_sem, reads=[self.x_tm], writes=[self.out_res])

    def final(self, b, t):
        self.stage = "final"
        o = self.buf([NSUB, D], F32)
        junk = self.buf([D], BF16)
        for s in range(NSUB):
            xs = self.x_tm[:, s, :]
            c = 8 + s
            self.act(junk, xs, AF.Square, accum=self.ss[:, c:c + 1])
            self.act(self.vv[:, c:c + 1], self.ss[:, c:c + 1], AF.Ln, bias=EPS, scale=1.0 / D)
            self.act(self.rstd[:, c:c + 1], self.vv[:, c:c + 1], AF.Exp, scale=-0.5)
            self.ts("dve", o[:, s, :], xs, self.rstd[:, c:c + 1], None, ALU.mult)
            self.tt("dve", o[:, s, :], o[:, s, :], self.nfb, ALU.mult)
        xo = self.out_d[b, t * T:(t + 1) * T, :].rearrange("(s p) d -> p s d", p=128)
        self.dma("pool", xo, o, self.out_sem, reads=[o], writes=[self.out_res])
        self.free(o, junk)


def _fm(v, nchunk):
    return np.ascontiguousarray(np.asarray(v, np.float32).reshape(nchunk, 128).T)


def prep_shared(inp):
    f = lambda k: np.asarray(inp[k], np.float32)
    cst = np.zeros((128, NCST), np.float32)
    cst[:, C_BADA:C_BADA + 72] = _fm(f("b_ada")[0], 72)
    cst[:, C_NF:C_NF + 8] = _fm(f("norm_ffn1")[0], 8)
    cst[:, C_NF + 8:C_NF + 16] = _fm(f("norm_mix")[0], 8)
    cst[:, C_NF + 16:C_NF + 24] = _fm(f("norm_ffn2")[0], 8)
    cw = f("conv_w")[0]
    cst[:, C_CONVW:C_CONVW + 48] = cw.reshape(4, 12, 128).transpose(2, 0, 1).reshape(128, 48)
    cst[:, C_CONVB:C_CONVB + 12] = _fm(f("conv_b")[0], 12)
    cst[:, C_QKVN:C_QKVN + 3] = _fm(f("q_norm_w")[0], 3)
    cst[:, C_QKVN + 3:C_QKVN + 5] = _fm(f("kv_norm_w")[0], 2)
    cst[:, C_MLAW:C_MLAW + 8] = _fm(f("mla_norm_w")[0], 8)
    cst[:, C_SSDW:C_SSDW + 8] = _fm(f("ssd_norm_w")[0], 8)
    cst[:, C_DTB:C_DTB + 16] = f("dt_bias")[0][None, :]
    cst[:, C_ALOG:C_ALOG + 16] = f("a_log")[0][None, :]
    cst[:, C_DSK:C_DSK + 16] = f("d_skip")[0][None, :]
    cst[:, C_DFEAT:C_DFEAT + 8] = _fm(np.repeat(f("d_skip")[0], 64), 8)
    i16 = np.arange(16)
    invf = (10000.0 ** (-(2.0 * i16) / 32.0)).astype(np.float32)
    iv = np.zeros(128, np.float32)
    ph = np.full(128, np.pi / 2, np.float32)
    iv[0:16] = invf; iv[16:32] = invf; iv[32:48] = invf; iv[48:64] = invf
    ph[0:32] = np.pi / 2; ph[32:48] = np.pi; ph[48:64] = 0.0
    cst[:, C_INVF] = iv
    cst[:, C_PHASE] = ph
    cmat = np.zeros((128, NCM), np.float32)
    r = np.arange(128)
    cmat[:, M_ID:M_ID + 128] = np.eye(128)
    cmat[:, M_U:M_U + 128] = (r[:, None] <= r[None, :])
    cmat[:, M_SU:M_SU + 128] = (r[:, None] > r[None, :])
    cmat[:, M_ONE:M_ONE + 128] = 1.0
    rq = np.zeros((128, 96), np.float32)
    for m in range(64):
        rq[64 + m, m] = 1.0
    for i in range(32):
        rq[i, 64 + i] = 1.0
        rq[32 + i, 64 + i] = 1.0
    cmat[:, M_RQ:M_RQ + 96] = rq
    nfb = np.ascontiguousarray(np.broadcast_to(f("norm_final")[None, :], (128, D)))
    wada = np.ascontiguousarray(f("w_ada")[0].reshape(KD, 128, 18, 512).transpose(2, 1, 0, 3).reshape(18, 128, PIECE))
    wbig = np.zeros((NPIECE, 128, PIECE), np.float32)

    def ffn_pieces(base, wg, wu, wd):
        g = wg.reshape(KD, 128, NHG, 256).transpose(2, 1, 0, 3)
        u = wu.reshape(KD, 128, NHG, 256).transpose(2, 1, 0, 3)
        gu = np.stack([g, u], axis=2)
        wbig[base:base + NHG] = gu.reshape(NHG, 128, PIECE)
        dd = wd.reshape(NHC, 128, D)
        for i in range(6):
            n = min(4, NHC - 4 * i)
            blk = dd[4 * i:4 * i + n].transpose(1, 0, 2).reshape(128, n * D)
            wbig[base + NHG + i, :, :n * D] = blk
    def kdp(cols):
        n = cols.shape[1]
        return cols.reshape(KD, 128, n).transpose(1, 0, 2).reshape(128, KD * n)
    win = f("w_in")[0]
    wbig[17] = kdp(win[:, 0:512])
    wbig[18] = kdp(win[:, 512:1024])
    wbig[19, :, :KD * 400] = kdp(np.concatenate([win[:, 2576:2960], win[:, 2560:2576]], axis=1))
    wbig[20, :, :KD * 256] = kdp(win[:, 2960:3216])
    for g in range(3):
        wbig[21 + g] = kdp(win[:, 1024 + 512 * g:1024 + 512 * (g + 1)])
    kr = win[:, 3216:3248]
    wbig[24, :, :KD * 64] = kdp(np.concatenate([kr[:, 0:16], kr[:, 16:32], kr[:, 16:32], kr[:, 0:16]], axis=1))
    wuq = f("w_uq")[0]
    ext = []
    for h in range(8):
        q = wuq[:, h * 96:(h + 1) * 96]
        r = q[:, 64:96]
        ext.append(np.concatenate([r[:, 0:16], r[:, 16:32], r[:, 16:32], r[:, 0:16], q[:, 0:64]], axis=1))
    ext = np.concatenate(ext, axis=1)
    wbig[25, :, :3072] = ext.reshape(3, 128, 1024).transpose(1, 0, 2).reshape(128, 3072)
    wukv = f("w_ukv")[0]
    z32 = np.zeros((256, 32), np.float32)
    wk = np.concatenate([np.concatenate([wukv[:, h * 192:h * 192 + 64], z32], axis=1) for h in range(8)], axis=1)
    wv = np.concatenate([wukv[:, h * 192 + 64:(h + 1) * 192] for h in range(8)], axis=1)
    wkv = np.concatenate([wk, wv], axis=1)
    wbig[26, :, :3584] = wkv.reshape(2, 128, 1792).transpose(1, 0, 2).reshape(128, 3584)
    wout = f("w_out")[0]
    for g in range(4):
        wbig[27 + g] = wout.reshape(4, 4, 128, D)[g].transpose(1, 0, 2).reshape(128, PIECE)
    ffn_pieces(0, f("ffn1_w_gate")[0], f("ffn1_w_up")[0], f("ffn1_w_down")[0])
    ffn_pieces(31, f("ffn2_w_gate")[0], f("ffn2_w_up")[0], f("ffn2_w_down")[0])
    return dict(cst=cst, cmat=cmat, nfb=nfb, wada=wada, wbig=wbig)


def prep_core(inp, shared, b0, NB):
    x = np.ascontiguousarray(np.asarray(inp["x"], np.float32)[b0:b0 + NB])
    pos = np.ascontiguousarray(np.asarray(inp["positions"], np.int32)[b0:b0 + NB])
    c = np.asarray(inp["c"], np.float32)[b0:b0 + NB]
    cT = np.ascontiguousarray(c.reshape(NB, KD, 128).transpose(2, 1, 0).reshape(128, KD * NB))
    m = dict(shared)
    m.update(x=x, pos=pos, cT=cT)
    return m


_CACHE = {}


def kernel(**inputs):
    S = inputs["x"].shape[1]
    NBT = inputs["x"].shape[0]
    ncores = 8
    NB = NBT // ncores
    key = (S, NB)
    if key not in _CACHE:
        _CACHE[key] = Prog(S, NB).build()
    nc = _CACHE[key]
    shared = prep_shared(inputs)
    in_maps = [prep_core(inputs, shared, i * NB, NB) for i in range(ncores)]
    res = run_bass_kernel_spmd(nc, in_maps, core_ids=list(range(ncores)))
    out = np.concatenate([np.asarray(r["out"]) for r in res.results], axis=0)
    return out.astype(np.float32, copy=False)
```
